# Optimizing a Trainium2 kernel written in Bass

```python
import math
import jax, jax.numpy as jnp
from jax import lax
import numpy as np

D_MODEL = 2048
BATCH = 4
SEQ = 2048
DEPTH = 1
DEC_BATCH = 4
DEC_SEQ = 4096
PAST_LEN = 128

N_MEM = 256
D_MIX = D_MODEL
D_DELTA = D_MIX // 2
D_FNET = D_MIX - D_DELTA
DN_HEADS = 8
DN_HEAD_DIM = D_DELTA // DN_HEADS
FN_GROUPS = 4
FN_GROUP_DIM = D_FNET // FN_GROUPS
CONV_WIDTH = 5
CHUNK = 64
XA_HEADS = 4
XA_HEAD_DIM = D_MODEL // XA_HEADS
D_FF = 5504
D_DN_IN = 4 * D_DELTA + 4 * DN_HEADS
D_IN = D_DN_IN + D_FNET
EPS = 1e-6

kernel_name = 'hybrid_deltanet_fnet_macaron_encoder'


def rms_norm(x, g):
    xf = x.astype(jnp.float32)
    y = xf * lax.rsqrt(jnp.mean(xf * xf, axis=-1, keepdims=True) + EPS)
    return (y * g.astype(jnp.float32)).astype(x.dtype)


def l2_normalize(t):
    return t * lax.rsqrt(jnp.sum(t * t, axis=-1, keepdims=True) + EPS)


def swiglu(h, w_gate, w_up, w_down):
    return (jax.nn.silu(h @ w_gate) * (h @ w_up)) @ w_down


def centred_short_conv(u, w):
    c = u.shape[-1]
    y = lax.conv_general_dilated(
        u, w[:, None, :].astype(u.dtype), window_strides=(1,),
        padding=[(CONV_WIDTH // 2, CONV_WIDTH // 2)],
        dimension_numbers=('NWC', 'WIO', 'NWC'), feature_group_count=c)
    return jax.nn.silu(y)


def gated_delta_rule_chunked(q, k, v, beta, g):
    b, h, s, dk = q.shape
    dv = v.shape[-1]
    n = s // CHUNK
    q = q.reshape(b, h, n, CHUNK, dk)
    k = k.reshape(b, h, n, CHUNK, dk)
    v = v.reshape(b, h, n, CHUNK, dv)
    beta = beta.reshape(b, h, n, CHUNK)
    cum_g = jnp.cumsum(g.reshape(b, h, n, CHUNK), axis=-1)
    incl = jnp.tril(jnp.ones((CHUNK, CHUNK), dtype=bool))
    strict = jnp.tril(jnp.ones((CHUNK, CHUNK), dtype=bool), k=-1)
    diff = cum_g[..., :, None] - cum_g[..., None, :]
    decay = jnp.where(incl, jnp.exp(jnp.where(incl, diff, 0.0)), 0.0)
    kb = k * beta[..., None]
    a_mat = jnp.where(strict, jnp.einsum('bhnid,bhnjd->bhnij', kb, k) * decay, 0.0)
    lhs = a_mat + jnp.eye(CHUNK, dtype=jnp.float32)
    rhs = jnp.concatenate([v * beta[..., None], kb * jnp.exp(cum_g)[..., None]], axis=-1)
    sol = lax.linalg.triangular_solve(lhs, rhs, left_side=True, lower=True, unit_diagonal=True)
    u_c, w_c = sol[..., :dv], sol[..., dv:]
    attn = jnp.einsum('bhnid,bhnjd->bhnij', q, k) * decay
    q_dec = q * jnp.exp(cum_g)[..., None]
    k_dec = k * jnp.exp(cum_g[..., -1:] - cum_g)[..., None]
    last = jnp.exp(cum_g[..., -1])

    def step(state, xs):
        qd, wc, uc, at, kd, gl = xs
        v_new = uc - jnp.einsum('bhik,bhkv->bhiv', wc, state)
        o = jnp.einsum('bhik,bhkv->bhiv', qd, state) + jnp.einsum('bhij,bhjv->bhiv', at, v_new)
        state = state * gl[..., None, None] + jnp.einsum('bhik,bhiv->bhkv', kd, v_new)
        return state, o

    xs = tuple(jnp.moveaxis(t, 2, 0) for t in (q_dec, w_c, u_c, attn, k_dec, last))
    s0 = jnp.zeros((b, h, dk, dv), jnp.float32)
    _, o = lax.scan(step, s0, xs)
    return jnp.moveaxis(o, 0, 2).reshape(b, h, s, dv)


def deltanet_mixer(p_dn, conv_w, a_log, dt_bias, head_norm):
    b, s, _ = p_dn.shape
    f32 = jnp.float32
    qkv = centred_short_conv(p_dn[..., :3 * D_DELTA], conv_w)
    z = p_dn[..., 3 * D_DELTA:4 * D_DELTA]
    off = 4 * D_DELTA
    beta_logit = p_dn[..., off:off + 2 * DN_HEADS].astype(f32).reshape(b, s, 2, DN_HEADS)
    a_logit = p_dn[..., off + 2 * DN_HEADS:off + 4 * DN_HEADS].astype(f32).reshape(b, s, 2, DN_HEADS)
    beta = jnp.transpose(jax.nn.sigmoid(beta_logit), (2, 0, 3, 1))
    g = -jnp.exp(a_log.astype(f32))[:, None, :, None] * jnp.transpose(
        jax.nn.softplus(a_logit + dt_bias.astype(f32)), (2, 0, 3, 1))

    def heads(t):
        return t.astype(f32).reshape(b, s, DN_HEADS, DN_HEAD_DIM).transpose(0, 2, 1, 3)

    q = l2_normalize(heads(qkv[..., :D_DELTA])) * (DN_HEAD_DIM ** -0.5)
    k = l2_normalize(heads(qkv[..., D_DELTA:2 * D_DELTA]))
    v = heads(qkv[..., 2 * D_DELTA:])
    o_fwd = gated_delta_rule_chunked(q, k, v, beta[0], g[0])

    def rev(t):
        return jnp.flip(t, axis=2)

    o_bwd = rev(gated_delta_rule_chunked(rev(q), rev(k), rev(v), rev(beta[1]), rev(g[1])))
    o = (o_fwd + o_bwd).transpose(0, 2, 1, 3)
    o = rms_norm(o, head_norm) * jax.nn.silu(z.astype(f32)).reshape(b, s, DN_HEADS, DN_HEAD_DIM)
    return o.reshape(b, s, D_DELTA).astype(p_dn.dtype)


def fourier_mixer(p_fn):
    b, s, _ = p_fn.shape
    u = p_fn.astype(jnp.float32).reshape(b, s, FN_GROUPS, FN_GROUP_DIM)
    y = jnp.real(jnp.fft.fft2(u, axes=(1, 3), norm='ortho'))
    return y.reshape(b, s, D_FNET).astype(p_fn.dtype)


def cross_attention(h, m, w_q, w_kv, w_o):
    b, s, _ = h.shape
    n_mem = m.shape[1]
    q = (h @ w_q).reshape(b, s, XA_HEADS, XA_HEAD_DIM)
    kv = (m @ w_kv).reshape(b, n_mem, 2, XA_HEADS, XA_HEAD_DIM)
    k, v = kv[:, :, 0], kv[:, :, 1]
    scores = jnp.einsum('bshd,bmhd->bhsm', q, k).astype(jnp.float32) * (XA_HEAD_DIM ** -0.5)
    probs = jax.nn.softmax(scores, axis=-1).astype(h.dtype)
    o = jnp.einsum('bhsm,bmhd->bshd', probs, v).reshape(b, s, D_MODEL)
    return o @ w_o


def encoder_layer(x, mem, ffn1_norm, ffn1_w_gate, ffn1_w_up, ffn1_w_down, mix_norm, w_in,
                  conv_w, a_log, dt_bias, dn_head_norm, w_out, xattn_norm, mem_norm,
                  xattn_w_q, xattn_w_kv, xattn_w_o, ffn2_norm, ffn2_w_gate, ffn2_w_up, ffn2_w_down):
    x = x + 0.5 * swiglu(rms_norm(x, ffn1_norm), ffn1_w_gate, ffn1_w_up, ffn1_w_down)
    p = rms_norm(x, mix_norm) @ w_in
    o_dn = deltanet_mixer(p[..., :D_DN_IN], conv_w, a_log, dt_bias, dn_head_norm)
    o_fn = fourier_mixer(p[..., D_DN_IN:])
    x = x + jnp.concatenate([o_dn, o_fn], axis=-1) @ w_out
    x = x + cross_attention(rms_norm(x, xattn_norm), rms_norm(mem, mem_norm),
                            xattn_w_q, xattn_w_kv, xattn_w_o)
    x = x + 0.5 * swiglu(rms_norm(x, ffn2_norm), ffn2_w_gate, ffn2_w_up, ffn2_w_down)
    return x


def encoder_trunk(x, mem, layer_weights, final_norm):
    for l in range(DEPTH):
        x = encoder_layer(x, mem, *[w[l] for w in layer_weights])
    return rms_norm(x, final_norm)


def setup_inputs(seed: int = 0) -> dict:
    key = jax.random.key(seed)
    ks = jax.random.split(key, 25)
    f32 = jnp.float32

    def nrm(k, shape, fan_in):
        return jax.random.normal(k, shape, f32) * (fan_in ** -0.5)

    def gain(k, shape):
        return 1.0 + 0.02 * jax.random.normal(k, shape, f32)

    dt = jnp.exp(jax.random.uniform(ks[12], (DEPTH, 2, DN_HEADS), f32,
                                    math.log(1e-3), math.log(1e-1)))
    return {
        'x_prompt': jax.random.normal(ks[0], (BATCH, SEQ, D_MODEL), f32),
        'x_sample': jax.random.normal(ks[1], (DEC_BATCH, DEC_SEQ, D_MODEL), f32),
        'mem_prompt': jax.random.normal(ks[2], (BATCH, N_MEM, D_MODEL), f32),
        'mem_sample': jax.random.normal(ks[3], (DEC_BATCH, N_MEM, D_MODEL), f32),
        'ffn1_norm': gain(ks[4], (DEPTH, D_MODEL)),
        'ffn1_w_gate': nrm(ks[5], (DEPTH, D_MODEL, D_FF), D_MODEL),
        'ffn1_w_up': nrm(ks[6], (DEPTH, D_MODEL, D_FF), D_MODEL),
        'ffn1_w_down': nrm(ks[7], (DEPTH, D_FF, D_MODEL), D_FF),
        'mix_norm': gain(ks[8], (DEPTH, D_MODEL)),
        'w_in': nrm(ks[9], (DEPTH, D_MODEL, D_IN), D_MODEL),
        'conv_w': nrm(ks[10], (DEPTH, CONV_WIDTH, 3 * D_DELTA), CONV_WIDTH),
        'a_log': jnp.log(jax.random.uniform(ks[11], (DEPTH, 2, DN_HEADS), f32, 1.0, 16.0)),
        'dt_bias': dt + jnp.log(-jnp.expm1(-dt)),
        'dn_head_norm': gain(ks[13], (DEPTH, DN_HEAD_DIM)),
        'w_out': nrm(ks[14], (DEPTH, D_MIX, D_MODEL), D_MIX),
        'xattn_norm': gain(ks[15], (DEPTH, D_MODEL)),
        'mem_norm': gain(ks[16], (DEPTH, D_MODEL)),
        'xattn_w_q': nrm(ks[17], (DEPTH, D_MODEL, D_MODEL), D_MODEL),
        'xattn_w_kv': nrm(ks[18], (DEPTH, D_MODEL, 2 * D_MODEL), D_MODEL),
        'xattn_w_o': nrm(ks[19], (DEPTH, D_MODEL, D_MODEL), D_MODEL),
        'ffn2_norm': gain(ks[20], (DEPTH, D_MODEL)),
        'ffn2_w_gate': nrm(ks[21], (DEPTH, D_MODEL, D_FF), D_MODEL),
        'ffn2_w_up': nrm(ks[22], (DEPTH, D_MODEL, D_FF), D_MODEL),
        'ffn2_w_down': nrm(ks[23], (DEPTH, D_FF, D_MODEL), D_FF),
        'final_norm': gain(ks[24], (D_MODEL,)),
    }


def reference(x_prompt, x_sample, mem_prompt, mem_sample, ffn1_norm, ffn1_w_gate, ffn1_w_up,
              ffn1_w_down, mix_norm, w_in, conv_w, a_log, dt_bias, dn_head_norm, w_out,
              xattn_norm, mem_norm, xattn_w_q, xattn_w_kv, xattn_w_o, ffn2_norm, ffn2_w_gate,
              ffn2_w_up, ffn2_w_down, final_norm):
    layer_weights = (ffn1_norm, ffn1_w_gate, ffn1_w_up, ffn1_w_down, mix_norm, w_in, conv_w,
                     a_log, dt_bias, dn_head_norm, w_out, xattn_norm, mem_norm, xattn_w_q,
                     xattn_w_kv, xattn_w_o, ffn2_norm, ffn2_w_gate, ffn2_w_up, ffn2_w_down)
    y_prompt = encoder_trunk(x_prompt, mem_prompt, layer_weights, final_norm)
    y_sample = encoder_trunk(x_sample, mem_sample, layer_weights, final_norm)
    return (y_prompt, y_sample)
```

```python
import os
import numpy as np
import ml_dtypes
from contextlib import ExitStack
import concourse.bass as bass
import concourse.mybir as mybir
from concourse.bass_utils import run_bass_kernel_spmd

F32 = mybir.dt.float32
BF16 = mybir.dt.bfloat16
AF = mybir.ActivationFunctionType
ALU = mybir.AluOpType
AX = mybir.AxisListType

D = 2048
DFF = 5504
DIN = 5152
NMEM = 256
EPS = 1e-6
NDMA = 40


class _Stop(Exception):
    pass


class Tracker:
    def __init__(self, nc):
        self.nc = nc
        self.engs = {"pe": nc.tensor, "act": nc.scalar, "dve": nc.vector, "pool": nc.gpsimd, "sp": nc.sync}
        self.sem = {k: nc.alloc_semaphore("s_" + k) for k in self.engs}
        self.cnt = {k: 0 for k in self.engs}
        self.waited = {k: {} for k in self.engs}
        self.lastw = {}
        self.readers = {}
        self.dma_sems = [nc.alloc_semaphore("s_dma%d" % i) for i in range(NDMA)]
        self.dma_cnt = [0] * NDMA
        self.dma_rr = 0

    def _wait(self, eng, tok):
        semkey, semobj, val = tok
        w = self.waited[eng]
        if w.get(semkey, 0) >= val:
            return
        self.engs[eng].wait_ge(semobj, val)
        w[semkey] = val

    def _deps(self, reads, writes):
        deps = []
        for k in reads:
            t = self.lastw.get(k)
            if t is not None:
                deps.append(t)
        for k in writes:
            t = self.lastw.get(k)
            if t is not None:
                deps.append(t)
            deps.extend(self.readers.get(k, {}).values())
        return deps

    def _update(self, tok, reads, writes):
        for k in reads:
            r = self.readers.setdefault(k, {})
            o = r.get(tok[0])
            if o is None or o[2] < tok[2]:
                r[tok[0]] = tok
        for k in writes:
            self.lastw[k] = tok
            self.readers[k] = {}

    def op(self, eng, fn, reads=(), writes=()):
        pr = [k for k in reads if isinstance(k, tuple) and len(k) == 2 and k[0] == "ps"]
        if pr:
            reads = [k for k in reads if k not in pr]
            writes = list(writes) + pr
        for t in self._deps(reads, writes):
            if eng == "pe" and t[0] == "pe":
                continue
            self._wait(eng, t)
        inst = fn(self.engs[eng])
        self.cnt[eng] += 1
        inst.then_inc(self.sem[eng], 1)
        tok = (eng, self.sem[eng], self.cnt[eng])
        self._update(tok, reads, writes)
        return tok

    def dma(self, eng, out, in_, reads=(), writes=()):
        i = self.dma_rr
        self.dma_rr = (i + 1) % NDMA
        key = ("dma", i)
        if self.dma_cnt[i] > 0:
            self._wait(eng, (key, self.dma_sems[i], self.dma_cnt[i]))
        for t in self._deps(reads, writes):
            self._wait(eng, t)
        self.engs[eng].dma_start(out=out, in_=in_).then_inc(self.dma_sems[i], 16)
        self.dma_cnt[i] += 16
        tok = (key, self.dma_sems[i], self.dma_cnt[i])
        self._update(tok, reads, writes)
        return tok

    def barrier(self):
        toks = [(k, self.sem[k], self.cnt[k]) for k in self.engs if self.cnt[k] > 0]
        toks += [(("dma", i), self.dma_sems[i], self.dma_cnt[i]) for i in range(NDMA) if self.dma_cnt[i] > 0]
        for e in self.engs:
            for t in toks:
                if t[0] == e:
                    continue
                self._wait(e, t)
        self.lastw = {}
        self.readers = {}


def build(S, dbg=False, upto=99):
    box = {}
    try:
        _build(S, dbg, upto, box)
    except _Stop:
        box["T"].barrier()
    return box["nc"]


def _build(S, dbg, upto, box):
    def ck(v):
        if upto == v:
            raise _Stop()

    NT = S // 512
    NCH = S // 128
    NG = NCH // 4
    nc = bass.Bass("TRN2", target_bir_lowering=False)
    T = Tracker(nc)
    box["nc"] = nc
    box["T"] = T

    def dram_in(name, shape, dt=F32):
        return nc.dram_tensor(name, list(shape), dt, kind="ExternalInput").ap()

    def dram_tmp(name, shape, dt):
        return nc.dram_tensor(name, list(shape), dt, kind="ExternalOutput" if dbg else "Internal").ap()

    x_in = dram_in("x", [S, D])
    mem_in = dram_in("mem", [NMEM, D])
    w_f32 = {
        "wg1": dram_in("wg1", [D, DFF]), "wu1": dram_in("wu1", [D, DFF]), "wd1": dram_in("wd1", [DFF, D]),
        "win": dram_in("win", [D, DIN]), "wout": dram_in("wout", [D, D]), "wq": dram_in("wq", [D, D]),
        "wkv": dram_in("wkv", [D, 2 * D]), "wo": dram_in("wo", [D, D]),
        "wg2": dram_in("wg2", [D, DFF]), "wu2": dram_in("wu2", [D, DFF]), "wd2": dram_in("wd2", [DFF, D]),
    }
    gains_in = dram_in("gains", [128, 5 * 16])
    fgain_in = dram_in("fgain", [128, D])
    convw_in = dram_in("convw", [128, 24 * 5])
    alog_in = dram_in("alog", [128, 16])
    dtb_in = dram_in("dtb", [128, 16])
    hng_in = dram_in("hng", [128, 128])
    ccsc_in = dram_in("ccsc", [128, 2 * 512], BF16)
    cs_in = dram_in("cs", [S, S], BF16)
    ssn_in = dram_in("ssn", [S, S], BF16)
    masks_in = dram_in("masks", [128, 8 * 128])
    identb_in = dram_in("identb", [128, 128], BF16)
    tmask_in = dram_in("tmask", [128, S], BF16)
    y_out = nc.dram_tensor("y", [S, D], F32, kind="ExternalOutput").ap()

    w_bf = {k: dram_tmp(k + "_b", v.shape, BF16) for k, v in w_f32.items()}
    x1_s = dram_tmp("x1_s", [S, D], F32)
    pT_qkv = dram_tmp("pT_qkv", [3072, S], F32)
    pT_fn = dram_tmp("pT_fn", [1024, S], BF16)
    z_s = dram_tmp("z_s", [S, 1024], F32)
    gates_s = dram_tmp("gates_s", [S, 32], F32)
    oT_s = dram_tmp("oT_s", [2048, S], BF16)

    ps = [nc.alloc_psum_tensor("ps%d" % b, [128, 512], F32) for b in range(8)]

    def psk(b):
        return ("ps", b)

    def sb(name, shape, dt):
        return nc.alloc_sbuf_tensor("sb_" + name, list(shape), dt)

    gains = sb("gains", [128, 5, 16], F32)
    identb = sb("identb", [128, 128], BF16)
    eps_t = sb("eps_t", [128, 1], F32)
    one_t = sb("one_t", [128, 1], F32)
    T.dma("sp", gains[:].rearrange("p a b -> p (a b)"), gains_in, writes=["gains"])
    T.dma("sp", identb[:], identb_in, writes=["identb"])
    T.op("dve", lambda e: e.memset(eps_t[:], EPS), writes=["eps_t"])
    T.op("dve", lambda e: e.memset(one_t[:], 1.0), writes=["one_t"])

    wparts = {}

    def wkeys(name, c0=None, ncol=None):
        np_ = wparts.get(name, 1)
        if np_ == 1 or c0 is None:
            return [("wb", name, p) for p in range(np_)]
        pw = w_f32[name].shape[1] // np_
        return [("wb", name, p) for p in range(c0 // pw, (c0 + ncol - 1) // pw + 1)]

    def cast_w_cols(name, p, np_):
        src, dst = w_f32[name], w_bf[name]
        R, C = src.shape
        pw = C // np_
        assert pw <= 2048 and R <= 2048
        T.dma("pool", dst[:, p * pw:(p + 1) * pw], src[:, p * pw:(p + 1) * pw], writes=[("wb", name, p)])

    def cast_w(name):
        src, dst = w_f32[name], w_bf[name]
        R, C = src.shape
        w = C
        while w > 2048:
            w //= 2
        assert C % w == 0
        a = C // w
        sv = src.rearrange("r (a w) -> (r a) w", w=w)
        dv = dst.rearrange("r (a w) -> (r a) w", w=w)
        rows = R * a
        step = 2048
        for r0 in range(0, rows, step):
            r1 = min(rows, r0 + step)
            T.dma("pool", dv[r0:r1, :], sv[r0:r1, :], writes=[("wb", name, 0)])

    wparts["wg1"] = 4
    wparts["wu1"] = 4
    for p in range(4):
        cast_w_cols("wg1", p, 4)
        cast_w_cols("wu1", p, 4)
    for name in ["wd1", "win", "wkv", "wout", "wq", "wo", "wg2", "wu2", "wd2"]:
        cast_w(name)

    if upto == 0:
        T.barrier()
        return nc
    def norm_transpose(xt, gi, hb, hT, ss, rs, junk):
        for ts in range(4):
            T.op("act", lambda e, ts=ts: e.activation(out=hb[:, ts, :], in_=xt[:, ts, :], func=AF.Square,
                                                      accum_out=ss[:, ts:ts + 1]),
                 reads=[("xt", ts)], writes=[("hb", ts), ("ss", ts)])
            T.op("act", lambda e, ts=ts: e.activation(out=rs[:, ts:ts + 1], in_=ss[:, ts:ts + 1], func=AF.Ln,
                                                      scale=1.0 / D, bias=eps_t[:, 0:1]),
                 reads=[("ss", ts), "eps_t"], writes=[("rs", ts)])
            T.op("act", lambda e, ts=ts: e.activation(out=rs[:, ts:ts + 1], in_=rs[:, ts:ts + 1], func=AF.Exp,
                                                      scale=-0.5),
                 reads=[("rs", ts)], writes=[("rs", ts)])
            T.op("dve", lambda e, ts=ts: e.tensor_scalar(out=hb[:, ts, :], in0=xt[:, ts, :],
                                                         scalar1=rs[:, ts:ts + 1], scalar2=None, op0=ALU.mult),
                 reads=[("xt", ts), ("rs", ts)], writes=[("hb", ts)])
        for fc in range(16):
            b = 4 + (fc % 4)
            pv = ps[b][:].bitcast(BF16)
            for ts in range(4):
                T.op("pe", lambda e, ts=ts, fc=fc, pv=pv: e.transpose(
                    out=pv[:, ts * 128:(ts + 1) * 128], in_=hb[:, ts, fc * 128:(fc + 1) * 128], identity=identb[:]),
                    reads=[("hb", ts), "identb"], writes=[psk(b)])
            if fc % 2 == 0:
                T.op("act", lambda e, fc=fc, pv=pv: e.activation(out=hT[:, fc, :], in_=pv[:, 0:512], func=AF.Copy,
                                                                 scale=gains[:, gi, fc:fc + 1]),
                     reads=[psk(b), "gains"], writes=[("hT", fc)])
            else:
                T.op("dve", lambda e, fc=fc, pv=pv: e.tensor_scalar(out=hT[:, fc, :], in0=pv[:, 0:512],
                                                                    scalar1=gains[:, gi, fc:fc + 1], scalar2=None,
                                                                    op0=ALU.mult),
                     reads=[psk(b), "gains"], writes=[("hT", fc)])

    def wtile_ap(wname, c0, ncols):
        return w_bf[wname].rearrange("(kc ki) f -> ki kc f", ki=128)[:, :, c0:c0 + ncols]

    def ffn(xt, gi, wg, wu, wd, bufs):
        hb, hT, ss, rs, junk, act, wgu, wdt, sg = bufs
        norm_transpose(xt, gi, hb, hT, ss, rs, junk)
        nft = (DFF + 255) // 256

        def load(ft):
            c0 = ft * 256
            ncol = min(256, DFF - c0)
            bi = ft % 2
            T.dma("sp", wgu[:, bi, 0, :, 0:ncol], wtile_ap(wg, c0, ncol), reads=wkeys(wg, c0, ncol), writes=[("wgu", bi, 0)])
            T.dma("sp", wgu[:, bi, 1, :, 0:ncol], wtile_ap(wu, c0, ncol), reads=wkeys(wu, c0, ncol), writes=[("wgu", bi, 1)])

        def comp(ft):
            c0 = ft * 256
            ncol = min(256, DFF - c0)
            bi = ft % 2
            for c in range(ncol // 128):
                fc = ft * 2 + c
                bg, bu = fc % 2, 2 + fc % 2
                for kc in range(16):
                    T.op("pe", lambda e, kc=kc, c=c: e.matmul(ps[bg][:], lhsT=wgu[:, bi, 0, kc, c * 128:(c + 1) * 128],
                                                              rhs=hT[:, kc, :], start=(kc == 0), stop=(kc == 15)),
                         reads=[("wgu", bi, 0), ("hT", kc)], writes=[psk(bg)])
                for kc in range(16):
                    T.op("pe", lambda e, kc=kc, c=c: e.matmul(ps[bu][:], lhsT=wgu[:, bi, 1, kc, c * 128:(c + 1) * 128],
                                                              rhs=hT[:, kc, :], start=(kc == 0), stop=(kc == 15)),
                         reads=[("wgu", bi, 1), ("hT", kc)], writes=[psk(bu)])
                si = fc % 2
                T.op("act", lambda e: e.activation(out=sg[:, si, :], in_=ps[bg][:], func=AF.Silu),
                     reads=[psk(bg)], writes=[("sg", si)])
                T.op("dve", lambda e, fc=fc: e.tensor_tensor(out=act[:, fc, :], in0=ps[bu][:], in1=sg[:, si, :],
                                                             op=ALU.mult),
                     reads=[psk(bu), ("sg", si)], writes=[("act", fc)])

        load(0)
        if nft > 1:
            load(1)
        for ft in range(nft):
            comp(ft)
            if ft + 2 < nft:
                load(ft + 2)
        nfc = DFF // 128
        groups = [(g0, min(8, nfc - g0)) for g0 in range(0, nfc, 8)]
        steps = [(dg, gi2) for dg in range(4) for gi2 in range(len(groups))]

        def loadd(i):
            dg, g = steps[i]
            g0, gn = groups[g]
            bi = i % 2
            src = w_bf[wd].rearrange("(fc fi) d -> fi fc d", fi=128)[:, g0:g0 + gn, dg * 512:(dg + 1) * 512]
            T.dma("sp", wdt[:, bi, 0:gn, :], src, reads=wkeys(wd), writes=[("wdt", bi)])

        def compd(i):
            dg, g = steps[i]
            g0, gn = groups[g]
            bi = i % 2
            for j in range(gn):
                fc = g0 + j
                for ts in range(4):
                    T.op("pe", lambda e, j=j, fc=fc, ts=ts: e.matmul(
                        ps[4 + ts][:], lhsT=act[:, fc, ts * 128:(ts + 1) * 128], rhs=wdt[:, bi, j, :],
                        start=(fc == 0), stop=(fc == nfc - 1)),
                        reads=[("act", fc), ("wdt", bi)], writes=[psk(4 + ts)])
            if g == len(groups) - 1:
                for ts in range(4):
                    T.op("dve", lambda e, ts=ts, dg=dg: e.scalar_tensor_tensor(
                        out=xt[:, ts, dg * 512:(dg + 1) * 512], in0=ps[4 + ts][:], scalar=0.5,
                        in1=xt[:, ts, dg * 512:(dg + 1) * 512], op0=ALU.mult, op1=ALU.add),
                        reads=[psk(4 + ts), ("xt", ts)], writes=[("xt", ts)])

        loadd(0)
        loadd(1)
        for i in range(len(steps)):
            compd(i)
            if i + 2 < len(steps):
                loadd(i + 2)

    def proj_tokmajor(xt, lhs_of, lhs_keys, wname, wdt, resid=True):
        steps = [(dg, g) for dg in range(4) for g in range(2)]

        def loadd(i):
            dg, g = steps[i]
            bi = i % 2
            src = w_bf[wname].rearrange("(kc ki) d -> ki kc d", ki=128)[:, g * 8:(g + 1) * 8, dg * 512:(dg + 1) * 512]
            T.dma("sp", wdt[:, bi, 0:8, :], src, reads=wkeys(wname), writes=[("wdt", bi)])

        def compd(i):
            dg, g = steps[i]
            bi = i % 2
            for j in range(8):
                kc = g * 8 + j
                for ts in range(4):
                    T.op("pe", lambda e, j=j, kc=kc, ts=ts: e.matmul(
                        ps[4 + ts][:], lhsT=lhs_of(kc, ts), rhs=wdt[:, bi, j, :], start=(kc == 0), stop=(kc == 15)),
                        reads=[lhs_keys(kc), ("wdt", bi)], writes=[psk(4 + ts)])
            if g == 1:
                for ts in range(4):
                    T.op("dve", lambda e, ts=ts, dg=dg: e.tensor_tensor(
                        out=xt[:, ts, dg * 512:(dg + 1) * 512], in0=ps[4 + ts][:],
                        in1=xt[:, ts, dg * 512:(dg + 1) * 512], op=ALU.add),
                        reads=[psk(4 + ts), ("xt", ts)], writes=[("xt", ts)])

        loadd(0)
        loadd(1)
        for i in range(len(steps)):
            compd(i)
            if i + 2 < len(steps):
                loadd(i + 2)

    with ExitStack() as es:
        def sbt(name, shape, dt):
            return es.enter_context(nc.sbuf_tensor("sb_" + name, list(shape), dt))
        xt = sbt("xt", [128, 4, D], F32)
        hb = sbt("hb", [128, 4, D], BF16)
        hT = sbt("hT", [128, 16, 512], BF16)
        ss = sbt("ss", [128, 4], F32)
        rs = sbt("rs", [128, 4], F32)
        junk = None
        act = sbt("act", [128, DFF // 128, 512], BF16)
        wgu = sbt("wgu", [128, 2, 2, 16, 256], BF16)
        wdt = sbt("wdt", [128, 2, 8, 512], BF16)
        sg = sbt("sg", [128, 2, 512], F32)
        stf = sbt("stf", [128, 2, 512], F32)
        stb = sbt("stb", [128, 2, 512], BF16)
        zst = sbt("zst", [128, 4, 1024], F32)
        gst = sbt("gst", [128, 4, 32], F32)
        wgt = sbt("wgt", [128, 16, 32], BF16)
        bufs = (hb, hT, ss, rs, junk, act, wgu, wdt, sg)
        T.dma("sp", wgt[:], wtile_ap("win", 4096, 32), reads=wkeys("win"), writes=["wgt"])

        for t in range(NT):
            t0 = t * 512
            for ts in range(4):
                T.dma("sp", xt[:, ts, :], x_in[t0 + ts * 128:t0 + (ts + 1) * 128, :], writes=[("xt", ts)])
            ffn(xt, 0, "wg1", "wu1", "wd1", bufs)
            for ts in range(4):
                T.dma("sp", x1_s[t0 + ts * 128:t0 + (ts + 1) * 128, :], xt[:, ts, :], reads=[("xt", ts)],
                      writes=[("x1s", t, ts)])
            norm_transpose(xt, 1, hb, hT, ss, rs, junk)
            tiles = [("qkv", c0) for c0 in range(0, 3072, 256)] + [("fn", c0) for c0 in range(4128, 5152, 256)] + \
                    [("z", c0) for c0 in range(3072, 4096, 256)]

            def loadw(i):
                kind, c0 = tiles[i]
                bi = i % 2
                T.dma("sp", wgu[:, bi, 0, :, :], wtile_ap("win", c0, 256), reads=wkeys("win"),
                      writes=[("wgu", bi, 0)])

            def compw(i):
                kind, c0 = tiles[i]
                bi = i % 2
                if kind in ("qkv", "fn"):
                    for c in range(2):
                        b = c
                        for kc in range(16):
                            T.op("pe", lambda e, kc=kc, c=c, b=b: e.matmul(
                                ps[b][:], lhsT=wgu[:, bi, 0, kc, c * 128:(c + 1) * 128], rhs=hT[:, kc, :],
                                start=(kc == 0), stop=(kc == 15)),
                                reads=[("wgu", bi, 0), ("hT", kc)], writes=[psk(b)])
                        if kind == "qkv":
                            r0 = c0 + c * 128
                            T.op("act", lambda e, c=c, b=b: e.copy(out=stf[:, c, :], in_=ps[b][:]),
                                 reads=[psk(b)], writes=[("stf", c)])
                            T.dma("sp", pT_qkv[r0:r0 + 128, t0:t0 + 512], stf[:, c, :], reads=[("stf", c)],
                                  writes=[("pTq", r0, t)])
                        else:
                            r0 = c0 - 4128 + c * 128
                            T.op("dve", lambda e, c=c, b=b: e.tensor_copy(out=stb[:, c, :], in_=ps[b][:]),
                                 reads=[psk(b)], writes=[("stb", c)])
                            T.dma("sp", pT_fn[r0:r0 + 128, t0:t0 + 512], stb[:, c, :], reads=[("stb", c)],
                                  writes=[("pTf", r0, t)])
                else:
                    zc = c0 - 3072
                    for ts in range(4):
                        b = 2 + ts % 2
                        for kc in range(16):
                            T.op("pe", lambda e, kc=kc, ts=ts, b=b: e.matmul(
                                ps[b][:, 0:256], lhsT=hT[:, kc, ts * 128:(ts + 1) * 128], rhs=wgu[:, bi, 0, kc, :],
                                start=(kc == 0), stop=(kc == 15)),
                                reads=[("wgu", bi, 0), ("hT", kc)], writes=[psk(b)])
                        T.op("act" if ts % 2 else "dve",
                             (lambda e, ts=ts, b=b: e.copy(out=zst[:, ts, zc:zc + 256], in_=ps[b][:, 0:256])) if ts % 2
                             else (lambda e, ts=ts, b=b: e.tensor_copy(out=zst[:, ts, zc:zc + 256], in_=ps[b][:, 0:256])),
                             reads=[psk(b)], writes=[("zst", ts)])

            loadw(0)
            loadw(1)
            for i in range(len(tiles)):
                compw(i)
                if i + 2 < len(tiles):
                    loadw(i + 2)
            for ts in range(4):
                b = 2 + ts % 2
                for kc in range(16):
                    T.op("pe", lambda e, kc=kc, ts=ts, b=b: e.matmul(
                        ps[b][:, 0:32], lhsT=hT[:, kc, ts * 128:(ts + 1) * 128], rhs=wgt[:, kc, :],
                        start=(kc == 0), stop=(kc == 15)),
                        reads=["wgt", ("hT", kc)], writes=[psk(b)])
                T.op("dve", lambda e, ts=ts, b=b: e.tensor_copy(out=gst[:, ts, :], in_=ps[b][:, 0:32]),
                     reads=[psk(b)], writes=[("gst", ts)])
            T.dma("sp", z_s[t0:t0 + 512, :].rearrange("(ts p) c -> p ts c", p=128), zst[:],
                  reads=[("zst", ts) for ts in range(4)], writes=[("zs", t)])
            T.dma("sp", gates_s[t0:t0 + 512, :].rearrange("(ts p) c -> p ts c", p=128), gst[:],
                  reads=[("gst", ts) for ts in range(4)], writes=[("gs", t)])
    T.barrier()
    if upto == 1:
        return nc

    with ExitStack() as es:
        def sbt(name, shape, dt):
            return es.enter_context(nc.sbuf_tensor("sb_" + name, list(shape), dt))
        masks = sbt("masks", [128, 8, 128], F32)
        T.dma("sp", masks[:].rearrange("p a b -> p (a b)"), masks_in, writes=["masks"])
        onesb = sbt("onesb", [128, 128], BF16)
        T.op("dve", lambda e: e.memset(onesb[:], 1.0), writes=["onesb"])
        convw = sbt("convw", [128, 24, 5], F32)
        T.dma("sp", convw[:].rearrange("p a b -> p (a b)"), convw_in, writes=["convw"])
        hng = sbt("hng", [128, 128], F32)
        T.dma("sp", hng[:], hng_in, writes=["hng"])
        gat = sbt("gat", [128, NCH, 32], F32)
        T.dma("sp", gat[:], gates_s.rearrange("(n p) c -> p n c", p=128), writes=["gat"])
        alog = sbt("alog", [128, 16], F32)
        dtb = sbt("dtb", [128, 16], F32)
        T.dma("sp", alog[:], alog_in, writes=["alog"])
        T.dma("sp", dtb[:], dtb_in, writes=["dtb"])
        beta = sbt("beta", [128, NCH, 16], F32)
        nbeta = sbt("nbeta", [128, NCH, 16], F32)
        lnbeta = sbt("lnbeta", [128, NCH, 16], F32)
        gg = sbt("gg", [128, NCH, 16], F32)
        cc = sbt("cc", [128, NCH, 16], F32)
        ec = sbt("ec", [128, NCH, 16], F32)
        ecl = sbt("ecl", [128, NCH, 16], F32)
        gl = sbt("gl", [128, NCH, 16], F32)
        T.op("act", lambda e: e.activation(out=beta[:], in_=gat[:, :, 0:16], func=AF.Sigmoid),
             reads=["gat"], writes=["beta"])
        T.op("dve", lambda e: e.tensor_scalar(out=nbeta[:], in0=beta[:], scalar1=-1.0, scalar2=None, op0=ALU.mult),
             reads=["beta"], writes=["nbeta"])
        T.op("act", lambda e: e.activation(out=lnbeta[:], in_=beta[:], func=AF.Ln), reads=["beta"], writes=["lnbeta"])
        T.op("dve", lambda e: e.tensor_tensor(out=gg[:], in0=gat[:, :, 16:32],
                                              in1=dtb[:].unsqueeze(1).broadcast_to([128, NCH, 16]), op=ALU.add),
             reads=["gat", "dtb"], writes=["gg"])
        T.op("act", lambda e: e.activation(out=gg[:], in_=gg[:], func=AF.Exp), reads=["gg"], writes=["gg"])
        T.op("act", lambda e: e.activation(out=gg[:], in_=gg[:], func=AF.Ln, bias=one_t[:, 0:1], scale=1.0),
             reads=["gg", "one_t"], writes=["gg"])
        T.op("act", lambda e: e.activation(out=alog[:], in_=alog[:], func=AF.Exp), reads=["alog"], writes=["alog"])
        T.op("dve", lambda e: e.scalar_tensor_tensor(out=gg[:], in0=gg[:], scalar=-1.0,
                                                     in1=alog[:].unsqueeze(1).broadcast_to([128, NCH, 16]),
                                                     op0=ALU.mult, op1=ALU.mult),
             reads=["gg", "alog"], writes=["gg"])
        ncol = NCH * 8
        for d in range(2):
            for c0 in range(0, NCH, 32):
                cn = min(32, NCH - c0)
                T.op("pe", lambda e, d=d, c0=c0, cn=cn: e.matmul(
                    ps[d][:, 0:cn * 8].rearrange("p (n h) -> p n h", h=8), lhsT=masks[:, d, :],
                    rhs=gg[:, c0:c0 + cn, d * 8:(d + 1) * 8], start=True, stop=True),
                    reads=["masks", "gg"], writes=[psk(d)])
                T.op("dve", lambda e, d=d, c0=c0, cn=cn: e.tensor_copy(
                    out=cc[:, c0:c0 + cn, d * 8:(d + 1) * 8],
                    in_=ps[d][:, 0:cn * 8].rearrange("p (n h) -> p n h", h=8)),
                    reads=[psk(d)], writes=["cc"])
        for c0 in range(0, NCH, 32):
            cn = min(32, NCH - c0)
            T.op("pe", lambda e, c0=c0, cn=cn: e.matmul(
                ps[2][:, 0:cn * 16].rearrange("p (n h) -> p n h", h=16), lhsT=masks[:, 2, :],
                rhs=gg[:, c0:c0 + cn, :], start=True, stop=True),
                reads=["masks", "gg"], writes=[psk(2)])
            T.op("act", lambda e, c0=c0, cn=cn: e.activation(
                out=gl[:, c0:c0 + cn, :], in_=ps[2][:, 0:cn * 16].rearrange("p (n h) -> p n h", h=16), func=AF.Exp),
                reads=[psk(2)], writes=["gl"])
            T.op("dve", lambda e, c0=c0, cn=cn: e.tensor_tensor(
                out=ecl[:, c0:c0 + cn, :], in0=ps[2][:, 0:cn * 16].rearrange("p (n h) -> p n h", h=16),
                in1=cc[:, c0:c0 + cn, :], op=ALU.subtract),
                reads=[psk(2), "cc"], writes=["ecl"])
        T.op("act", lambda e: e.activation(out=ecl[:], in_=ecl[:], func=AF.Exp), reads=["ecl"], writes=["ecl"])
        T.op("act", lambda e: e.activation(out=ec[:], in_=cc[:], func=AF.Exp), reads=["cc"], writes=["ec"])

        ck(1.1)
        with ExitStack() as es2:
            def sbt2(name, shape, dt):
                return es2.enter_context(nc.sbuf_tensor("sb_" + name, list(shape), dt))
            TB = min(1024, S)
            raw = sbt2("raw", [128, 2, TB + 4], F32)
            cv = sbt2("cv", [128, 2, TB], F32)
            sq = sbt2("sq", [128, 2, 512], BF16)
            rinv = sbt2("rinv", [128, 2, 512], F32)
            tmask = sbt2("tmask", [128, S], BF16)
            T.dma("sp", tmask[:], tmask_in, writes=["tmask"])
            qT3 = sbt2("qT", [128, 2, S], BF16)
            kT3 = sbt2("kT", [128, 2, S], BF16)
            vT3 = sbt2("vT", [128, 2, S], BF16)
            ktok = sbt2("ktok", [128, NCH, 128], BF16)
            vtok = sbt2("vtok", [128, NCH, 128], BF16)
            oacc = sbt2("oacc", [128, NCH, 128], F32)
            zt = sbt2("zt", [128, 2, 4, 128], F32)
            ob = sbt2("ob", [128, 2, 4, 128], BF16)
            ssh = sbt2("ssh", [128, NCH], F32)
            junk2 = sbt2("junk2", [128, 128], F32)
            ost = sbt2("ost", [128, 2, 512], BF16)
            Sf = sbt2("Sf", [128, 2, 128], F32)
            Sb = sbt2("Sb", [128, 2, 128], BF16)
            rhsg = sbt2("rhsg", [128, 2, 4, 128], F32)
            DT = sbt2("DT", [128, 2, 4, 128], F32)
            EB = sbt2("EB", [128, 2, 512], F32)
            DTi = sbt2("DTi", [128, 2, 4, 128], F32)
            DTs = sbt2("DTs", [128, 2, 4, 128], F32)
            attnT = sbt2("attnT", [128, 2, 2, 512], BF16)
            PT = sbt2("PT", [128, 2, 2, 512], BF16)
            PP = sbt2("PP", [128, 2, 2, 512], BF16)
            XTb = sbt2("XTb", [128, 2, 512], BF16)
            identf = sbt2("identf", [128, 128], F32)
            keg = sbt2("keg", [128, 2, 4, 128], BF16)
            kdg = sbt2("kdg", [128, 2, 2, 4, 128], BF16)
            wT = sbt2("wT", [128, 2, 2, 512], BF16)
            ub = sbt2("ub", [128, 2, 2, 4, 128], F32)
            qdT = sbt2("qdT", [128, 2, 2, 512], BF16)
            vnew = sbt2("vnew", [128, 2, 128], BF16)
            T.op("act", lambda e: e.copy(out=identf[:], in_=identb[:]), reads=["identb"], writes=["identf"])

            def b1_task(hh):
                hb1 = hh % 2
                for wi, (dst3, scale, dk) in enumerate(((qT3, 128.0 ** -0.5, "qT"), (kT3, 1.0, "kT"), (vT3, None, "vT"))):
                    dst = dst3[:, hb1, :]
                    dkey = (dk, hb1)
                    r0 = wi * 1024 + hh * 128
                    cch = wi * 8 + hh
                    for tb in range(S // TB):
                        t0 = tb * TB
                        bi = (wi * (S // TB) + tb) % 2
                        lo = max(0, t0 - 2)
                        hi = min(S, t0 + TB + 2)
                        if t0 == 0:
                            T.op("pool", lambda e, bi=bi: e.memset(raw[:, bi, 0:2], 0.0), writes=[("raw", bi)])
                        if t0 + TB == S:
                            T.op("pool", lambda e, bi=bi: e.memset(raw[:, bi, TB + 2:TB + 4], 0.0),
                                 writes=[("raw", bi)])
                        T.dma("sp", raw[:, bi, lo - (t0 - 2):hi - (t0 - 2)], pT_qkv[r0:r0 + 128, lo:hi],
                              writes=[("raw", bi)])
                        yield
                        T.op("dve", lambda e, bi=bi, cch=cch: e.tensor_scalar(
                            out=cv[:, bi, :], in0=raw[:, bi, 0:TB], scalar1=convw[:, cch, 0:1], scalar2=None,
                            op0=ALU.mult), reads=[("raw", bi), "convw"], writes=[("cv", bi)])
                        yield
                        for j in range(1, 5):
                            T.op("dve", lambda e, bi=bi, cch=cch, j=j: e.scalar_tensor_tensor(
                                out=cv[:, bi, :], in0=raw[:, bi, j:j + TB], scalar=convw[:, cch, j:j + 1],
                                in1=cv[:, bi, :], op0=ALU.mult, op1=ALU.add),
                                reads=[("raw", bi), "convw", ("cv", bi)], writes=[("cv", bi)])
                            yield
                        T.op("pool", lambda e, bi=bi, t0=t0: e.tensor_tensor(
                            out=cv[:, bi, :], in0=cv[:, bi, :], in1=tmask[:, t0:t0 + TB], op=ALU.mult),
                            reads=[("cv", bi), "tmask"], writes=[("cv", bi)])
                        yield
                        if scale is None:
                            T.op("act", lambda e, bi=bi, t0=t0, dst=dst: e.activation(
                                out=dst[:, t0:t0 + TB], in_=cv[:, bi, :], func=AF.Silu),
                                reads=[("cv", bi)], writes=[dkey])
                            yield
                            continue
                        T.op("act", lambda e, bi=bi: e.activation(out=cv[:, bi, :], in_=cv[:, bi, :], func=AF.Silu),
                             reads=[("cv", bi)], writes=[("cv", bi)])
                        yield
                        for s5 in range(TB // 512):
                            si = s5 % 2
                            bk = 3
                            c5 = s5 * 512
                            T.op("pool", lambda e, bi=bi, si=si, c5=c5: e.tensor_tensor(
                                out=sq[:, si, :], in0=cv[:, bi, c5:c5 + 512], in1=cv[:, bi, c5:c5 + 512], op=ALU.mult),
                                reads=[("cv", bi)], writes=[("sq", si)])
                            yield
                            T.op("pe", lambda e, si=si, bk=bk: e.matmul(ps[bk][:], lhsT=onesb[:], rhs=sq[:, si, :],
                                                                        start=True, stop=True),
                                 reads=["onesb", ("sq", si)], writes=[psk(bk)])
                            yield
                            T.op("act", lambda e, si=si, bk=bk: e.activation(out=rinv[:, si, :], in_=ps[bk][:],
                                                                             func=AF.Ln, bias=eps_t[:, 0:1], scale=1.0),
                                 reads=[psk(bk), "eps_t"], writes=[("rinv", si)])
                            yield
                            T.op("act", lambda e, si=si: e.activation(out=rinv[:, si, :], in_=rinv[:, si, :],
                                                                      func=AF.Exp, scale=-0.5),
                                 reads=[("rinv", si)], writes=[("rinv", si)])
                            yield
                            T.op("dve", lambda e, bi=bi, si=si, c5=c5, t0=t0, dst=dst, scale=scale:
                                 e.scalar_tensor_tensor(out=dst[:, t0 + c5:t0 + c5 + 512], in0=cv[:, bi, c5:c5 + 512],
                                                        scalar=scale, in1=rinv[:, si, :], op0=ALU.mult, op1=ALU.mult),
                                 reads=[("cv", bi), ("rinv", si)], writes=[dkey])
                            yield

            for _ in b1_task(0):
                pass
            for h in range(8):
                hb = h % 2
                qT = qT3[:, hb, :]
                kT = kT3[:, hb, :]
                vT = vT3[:, hb, :]
                qTk, kTk, vTk = ("qT", hb), ("kT", hb), ("vT", hb)
                bg = b1_task(h + 1) if h + 1 < 8 else None
                for src, dstt, skey, dkey in ((kT, ktok, kTk, "ktok"), (vT, vtok, vTk, "vtok")):
                    for n0 in range(0, NCH, 4):
                        b = 2 + (n0 // 4) % 2
                        pv = ps[b][:].bitcast(BF16)
                        for nn in range(4):
                            n = n0 + nn
                            T.op("pe", lambda e, n=n, nn=nn, pv=pv, src=src: e.transpose(
                                out=pv[:, nn * 128:(nn + 1) * 128], in_=src[:, n * 128:(n + 1) * 128],
                                identity=identb[:]), reads=[skey, "identb"], writes=[psk(b)])
                        T.op("act", lambda e, n0=n0, pv=pv, dstt=dstt: e.copy(
                            out=dstt[:, n0:n0 + 4, :].rearrange("p a b -> p (a b)"), in_=pv[:, 0:512]),
                            reads=[psk(b)], writes=[dkey])
                ck(1.2)
                T.op("pool", lambda e: e.memset(oacc[:], 0.0), writes=[("oacc", n) for n in range(NCH)])
                T.op("pool", lambda e: e.memset(Sf[:], 0.0), writes=[("Sf", 0), ("Sf", 1)])
                T.op("pool", lambda e: e.memset(Sb[:], 0.0), writes=[("Sb", 0), ("Sb", 1)])

                def precompute(d, G, sbi):
                    dh = d * 8 + h
                    n0 = G * 4
                    c0 = n0 * 128
                    mI, mS, mLT = (0, 6, 3) if d == 0 else (1, 7, 4)
                    ba, bb, bc = (0, 1, 4) if d == 0 else (5, 6, 7)
                    gb = lambda t: t[:, n0:n0 + 4, dh:dh + 1].broadcast_to([128, 4, 128])
                    mb = lambda mi: masks[:, mi, :].unsqueeze(1).broadcast_to([128, 4, 128])
                    fl = lambda t: t.rearrange("p a b -> p (a b)")
                    T.op("pool", lambda e: e.tensor_tensor(out=rhsg[:, d], in0=mb(mI), in1=gb(gg), op=ALU.mult),
                         reads=["masks", "gg"], writes=[("rhsg", d)])
                    yield
                    T.op("pe", lambda e: e.matmul(ps[ba][:], lhsT=masks[:, mLT, :], rhs=fl(rhsg[:, d]), start=True,
                                                  stop=True), reads=["masks", ("rhsg", d)], writes=[psk(ba)])
                    T.op("pe", lambda e: e.matmul(ps[bb][:], lhsT=masks[:, 2, :], rhs=fl(rhsg[:, d]), start=True,
                                                  stop=True), reads=["masks", ("rhsg", d)], writes=[psk(bb)])
                    for nn in range(4):
                        n = n0 + nn
                        T.op("pe", lambda e, n=n, nn=nn: e.matmul(
                            ps[bc][:, nn * 128:(nn + 1) * 128], lhsT=kT[:, n * 128:(n + 1) * 128],
                            rhs=kT[:, n * 128:(n + 1) * 128], start=True, stop=True),
                            reads=[kTk], writes=[psk(bc)])
                    yield
                    for nn in range(4):
                        T.op("act", lambda e, nn=nn: e.activation(
                            out=DTs[:, d, nn, :], in_=ps[ba][:, nn * 128:(nn + 1) * 128], func=AF.Exp,
                            bias=lnbeta[:, n0 + nn, dh:dh + 1], scale=1.0),
                            reads=[psk(ba), "lnbeta"], writes=[("DTs", d)])
                    yield
                    T.op("act", lambda e: e.activation(out=fl(DT[:, d]), in_=ps[ba][:], func=AF.Exp),
                         reads=[psk(ba)], writes=[("DT", d)])
                    yield
                    T.op("pool", lambda e: e.tensor_tensor(out=DTs[:, d], in0=DTs[:, d], in1=mb(mS), op=ALU.mult),
                         reads=[("DTs", d), "masks"], writes=[("DTs", d)])
                    yield
                    for nn in range(4):
                        n = n0 + nn
                        T.op("pe", lambda e, n=n, nn=nn: e.matmul(
                            ps[ba][:, nn * 128:(nn + 1) * 128], lhsT=kT[:, n * 128:(n + 1) * 128],
                            rhs=qT[:, n * 128:(n + 1) * 128], start=True, stop=True),
                            reads=[kTk, qTk], writes=[psk(ba)])
                    yield
                    T.op("act", lambda e: e.activation(out=EB[:, d, :], in_=ps[bb][:], func=AF.Exp),
                         reads=[psk(bb)], writes=[("EB", d)])
                    yield
                    T.op("dve", lambda e: e.tensor_tensor(out=PT[:, d, 0, :], in0=ps[bc][:], in1=fl(DTs[:, d]),
                                                          op=ALU.mult),
                         reads=[psk(bc), ("DTs", d)], writes=[("PT", d, 0)])
                    yield
                    pv = ps[bb][:].bitcast(BF16)
                    for nn in range(4):
                        T.op("pe", lambda e, nn=nn: e.transpose(out=pv[:, nn * 128:(nn + 1) * 128],
                                                                in_=PT[:, d, 0, nn * 128:(nn + 1) * 128],
                                                                identity=identb[:]),
                             reads=[("PT", d, 0), "identb"], writes=[psk(bb)])
                    yield
                    T.op("pool", lambda e: e.tensor_tensor(out=DTi[:, d], in0=DT[:, d], in1=mb(mI), op=ALU.mult),
                         reads=[("DT", d), "masks"], writes=[("DTi", d)])
                    yield
                    T.op("act", lambda e: e.copy(out=PP[:, d, 0, :], in_=pv[:, 0:512]), reads=[psk(bb)],
                         writes=[("PP", d, 0)])
                    yield
                    T.op("pool", lambda e: e.tensor_tensor(
                        out=XTb[:, d, :].rearrange("p (a b) -> p a b", b=128),
                        in0=identf[:].unsqueeze(1).broadcast_to([128, 4, 128]),
                        in1=PT[:, d, 0, :].rearrange("p (a b) -> p a b", b=128), op=ALU.subtract),
                        reads=[("PT", d, 0), "identf"], writes=[("XTb", d)])
                    yield
                    T.op("dve", lambda e: e.tensor_tensor(out=attnT[:, d, sbi, :], in0=ps[ba][:], in1=fl(DTi[:, d]),
                                                          op=ALU.mult),
                         reads=[psk(ba), ("DTi", d)], writes=[("attnT", d, sbi)])
                    yield
                    NL = 6
                    for lv in range(1, NL + 1):
                        a, bq = (lv - 1) % 2, lv % 2
                        for nn in range(4):
                            sl = slice(nn * 128, (nn + 1) * 128)
                            T.op("pe", lambda e, sl=sl, a=a: e.matmul(ps[ba][:, sl], lhsT=PT[:, d, a, sl],
                                                                      rhs=PP[:, d, a, sl], start=True, stop=True),
                                 reads=[("PT", d, a), ("PP", d, a)], writes=[psk(ba)])
                        if lv < NL:
                            for nn in range(4):
                                sl = slice(nn * 128, (nn + 1) * 128)
                                T.op("pe", lambda e, sl=sl, a=a: e.matmul(ps[bb][:, sl], lhsT=PP[:, d, a, sl],
                                                                          rhs=PT[:, d, a, sl], start=True, stop=True),
                                     reads=[("PT", d, a), ("PP", d, a)], writes=[psk(bb)])
                        yield
                        T.op("act", lambda e, bq=bq: e.copy(out=PP[:, d, bq, :], in_=ps[ba][:]), reads=[psk(ba)],
                             writes=[("PP", d, bq)])
                        yield
                        if lv < NL:
                            T.op("dve", lambda e, bq=bq: e.tensor_copy(out=PT[:, d, bq, :], in_=ps[bb][:]),
                                 reads=[psk(bb)], writes=[("PT", d, bq)])
                            yield
                        for nn in range(4):
                            sl = slice(nn * 128, (nn + 1) * 128)
                            T.op("pe", lambda e, sl=sl, bq=bq: e.matmul(ps[bc][:, sl], lhsT=PP[:, d, bq, sl],
                                                                        rhs=XTb[:, d, sl], start=True, stop=True),
                                 reads=[("PP", d, bq), ("XTb", d)], writes=[psk(bc)])
                        yield
                        T.op("dve", lambda e: e.tensor_tensor(out=XTb[:, d, :], in0=ps[bc][:], in1=XTb[:, d, :],
                                                              op=ALU.add),
                             reads=[psk(bc), ("XTb", d)], writes=[("XTb", d)])
                        yield
                        if lv == 2:
                            T.op("pool", lambda e: e.tensor_tensor(out=keg[:, d], in0=ktok[:, n0:n0 + 4, :],
                                                                   in1=gb(ec), op=ALU.mult),
                                 reads=["ktok", "ec"], writes=[("keg", d)])
                            yield
                        if lv == 3:
                            T.op("pool", lambda e: e.tensor_tensor(out=kdg[:, d, sbi], in0=ktok[:, n0:n0 + 4, :],
                                                                   in1=gb(ecl), op=ALU.mult),
                                 reads=["ktok", "ecl"], writes=[("kdg", d, sbi)])
                            yield
                        if lv == 4:
                            T.op("pool", lambda e: e.tensor_tensor(out=qdT[:, d, sbi, :], in0=qT[:, c0:c0 + 512],
                                                                   in1=EB[:, d, :], op=ALU.mult),
                                 reads=[qTk, ("EB", d)], writes=[("qdT", d, sbi)])
                            yield
                    for nn in range(4):
                        sl = slice(nn * 128, (nn + 1) * 128)
                        T.op("pe", lambda e, sl=sl, nn=nn: e.matmul(ps[ba][:, sl], lhsT=XTb[:, d, sl],
                                                                    rhs=vtok[:, n0 + nn, :], start=True, stop=True),
                             reads=[("XTb", d), "vtok"], writes=[psk(ba)])
                    for nn in range(4):
                        sl = slice(nn * 128, (nn + 1) * 128)
                        T.op("pe", lambda e, sl=sl, nn=nn: e.matmul(ps[bb][:, sl], lhsT=keg[:, d, nn, :],
                                                                    rhs=XTb[:, d, sl], start=True, stop=True),
                             reads=[("XTb", d), ("keg", d)], writes=[psk(bb)])
                    yield
                    T.op("dve", lambda e: e.tensor_tensor(out=ub[:, d, sbi],
                                                          in0=ps[ba][:].rearrange("p (a b) -> p a b", b=128),
                                                          in1=gb(beta), op=ALU.mult),
                         reads=[psk(ba), "beta"], writes=[("ub", d, sbi)])
                    yield
                    T.op("act", lambda e: e.copy(out=wT[:, d, sbi, :], in_=ps[bb][:]), reads=[psk(bb)],
                         writes=[("wT", d, sbi)])
                    yield

                def scan_step(d, n, nn, sbi):
                    dh = d * 8 + h
                    sl = slice(nn * 128, (nn + 1) * 128)
                    rA = slice(d * 256, d * 256 + 128)
                    rB = slice(d * 256 + 128, d * 256 + 256)
                    sbk = 2
                    T.op("pe", lambda e: e.matmul(ps[sbk][:, rA], lhsT=wT[:, d, sbi, sl], rhs=Sb[:, d, :], start=True,
                                                  stop=True),
                         reads=[("wT", d, sbi), ("Sb", d)], writes=[psk(sbk)])
                    yield
                    T.op("dve", lambda e: e.scalar_tensor_tensor(out=vnew[:, d, :], in0=ps[sbk][:, rA],
                                                                 scalar=nbeta[:, n, dh:dh + 1], in1=ub[:, d, sbi, nn, :],
                                                                 op0=ALU.mult, op1=ALU.add),
                         reads=[psk(sbk), "nbeta", ("ub", d, sbi)], writes=[("vnew", d)])
                    yield
                    T.op("pe", lambda e: e.matmul(ps[sbk][:, rB], lhsT=qdT[:, d, sbi, sl], rhs=Sb[:, d, :], start=True,
                                                  stop=False),
                         reads=[("qdT", d, sbi), ("Sb", d)], writes=[psk(sbk)])
                    T.op("pe", lambda e: e.matmul(ps[sbk][:, rB], lhsT=attnT[:, d, sbi, sl], rhs=vnew[:, d, :],
                                                  start=False, stop=True),
                         reads=[("attnT", d, sbi), ("vnew", d)], writes=[psk(sbk)])
                    T.op("pe", lambda e: e.matmul(ps[sbk][:, rA], lhsT=kdg[:, d, sbi, nn, :], rhs=vnew[:, d, :],
                                                  start=True, stop=True),
                         reads=[("kdg", d, sbi), ("vnew", d)], writes=[psk(sbk)])
                    yield
                    T.op("dve", lambda e: e.scalar_tensor_tensor(out=Sf[:, d, :], in0=Sf[:, d, :],
                                                                 scalar=gl[:, n, dh:dh + 1], in1=ps[sbk][:, rA],
                                                                 op0=ALU.mult, op1=ALU.add),
                         reads=[psk(sbk), ("Sf", d), "gl"], writes=[("Sf", d)])
                    yield
                    T.op("act", lambda e: e.copy(out=Sb[:, d, :], in_=Sf[:, d, :]), reads=[("Sf", d)],
                         writes=[("Sb", d)])
                    yield
                    T.op("dve", lambda e: e.tensor_tensor(out=oacc[:, n, :], in0=ps[sbk][:, rB], in1=oacc[:, n, :],
                                                          op=ALU.add),
                         reads=[psk(sbk), ("oacc", n)], writes=[("oacc", n)])
                    yield

                def scan_group(d, s):
                    G = s if d == 0 else NG - 1 - s
                    for st in range(4):
                        nn = st if d == 0 else 3 - st
                        yield from scan_step(d, G * 4 + nn, nn, s % 2)

                def run_tasks(tasks, weights):
                    active = [(t, w) for t, w in zip(tasks, weights)]
                    while active:
                        if bg is not None:
                            next(bg, None)
                        for tw in list(active):
                            t, w = tw
                            for _ in range(w):
                                try:
                                    next(t)
                                except StopIteration:
                                    active.remove(tw)
                                    break

                run_tasks([precompute(0, 0, 0), precompute(1, NG - 1, 0)], [1, 1])
                ck(1.4)
                for s in range(NG):
                    tasks = [scan_group(0, s), scan_group(1, s)]
                    wts = [1, 1]
                    if s + 1 < NG:
                        tasks += [precompute(0, s + 1, (s + 1) % 2), precompute(1, NG - 2 - s, (s + 1) % 2)]
                        wts += [2, 2]
                    run_tasks(tasks, wts)
                if bg is not None:
                    for _ in bg:
                        pass

                ck(1.5)
                for n in range(NCH):
                    T.op("act", lambda e, n=n: e.activation(out=junk2[:], in_=oacc[:, n, :], func=AF.Square,
                                                            accum_out=ssh[:, n:n + 1]),
                         reads=[("oacc", n)], writes=["junk2", "ssh"])
                T.op("act", lambda e: e.activation(out=ssh[:], in_=ssh[:], func=AF.Ln, scale=1.0 / 128,
                                                   bias=eps_t[:, 0:1]), reads=["ssh", "eps_t"], writes=["ssh"])
                T.op("act", lambda e: e.activation(out=ssh[:], in_=ssh[:], func=AF.Exp, scale=-0.5),
                     reads=["ssh"], writes=["ssh"])
                for n0 in range(0, NCH, 4):
                    gi4 = (n0 // 4) % 2
                    og = [("oacc", n0 + nn) for nn in range(4)]
                    T.dma("sp", zt[:, gi4], z_s[n0 * 128:(n0 + 4) * 128, h * 128:(h + 1) * 128].rearrange(
                        "(n p) c -> p n c", p=128), writes=[("zt", gi4)])
                    T.op("act", lambda e, gi4=gi4: e.activation(out=zt[:, gi4], in_=zt[:, gi4], func=AF.Silu),
                         reads=[("zt", gi4)], writes=[("zt", gi4)])
                    T.op("pool", lambda e, gi4=gi4: e.tensor_tensor(
                        out=zt[:, gi4], in0=zt[:, gi4], in1=hng[:].unsqueeze(1).broadcast_to([128, 4, 128]),
                        op=ALU.mult), reads=[("zt", gi4), "hng"], writes=[("zt", gi4)])
                    T.op("dve", lambda e, n0=n0: e.tensor_tensor(
                        out=oacc[:, n0:n0 + 4, :], in0=oacc[:, n0:n0 + 4, :],
                        in1=ssh[:, n0:n0 + 4].unsqueeze(2).broadcast_to([128, 4, 128]), op=ALU.mult),
                        reads=og + ["ssh"], writes=og)
                    T.op("dve", lambda e, n0=n0, gi4=gi4: e.tensor_tensor(
                        out=ob[:, gi4], in0=oacc[:, n0:n0 + 4, :], in1=zt[:, gi4], op=ALU.mult),
                        reads=og + [("zt", gi4)], writes=[("ob", gi4)])
                    b = 2 + gi4
                    pv = ps[b][:].bitcast(BF16)
                    for nn in range(4):
                        T.op("pe", lambda e, nn=nn, gi4=gi4, pv=pv: e.transpose(
                            out=pv[:, nn * 128:(nn + 1) * 128], in_=ob[:, gi4, nn, :], identity=identb[:]),
                            reads=[("ob", gi4), "identb"], writes=[psk(b)])
                    T.op("act", lambda e, gi4=gi4, pv=pv: e.copy(out=ost[:, gi4, :], in_=pv[:, 0:512]),
                         reads=[psk(b)], writes=[("ost", gi4)])
                    T.dma("sp", oT_s[h * 128:(h + 1) * 128, n0 * 128:n0 * 128 + 512], ost[:, gi4, :],
                          reads=[("ost", gi4)], writes=[("oTs", h, n0)])
                ck(1.6)
        T.barrier()
        ck(1.7)

        with ExitStack() as es2:
            def sbt2(name, shape, dt):
                return es2.enter_context(nc.sbuf_tensor("sb_" + name, list(shape), dt))
            ccsc = sbt2("ccsc", [128, 2, 512], BF16)
            T.dma("sp", ccsc[:].rearrange("p a b -> p (a b)"), ccsc_in, writes=["ccsc"])
            UT = sbt2("UT", [128, 2, S], BF16)
            AB = sbt2("AB", [128, NCH, 512], BF16)
            HB = max(1, NCH // 2)
            blk = sbt2("blk", [128, 4, HB, 512], BF16)
            yst = sbt2("yst", [128, 2, 512], BF16)
            for g in range(4):
                for c in range(2):
                    T.dma("sp", UT[:, c, :], pT_fn[g * 256 + c * 128:g * 256 + (c + 1) * 128, :], writes=[("UT", c)])
                for n in range(NCH):
                    b = n % 2
                    for c in range(2):
                        T.op("pe", lambda e, n=n, c=c, b=b: e.matmul(ps[b][:], lhsT=UT[:, c, n * 128:(n + 1) * 128],
                                                                     rhs=ccsc[:, c, :], start=(c == 0), stop=(c == 1)),
                             reads=[("UT", c), "ccsc"], writes=[psk(b)])
                    if n % 2:
                        T.op("act", lambda e, n=n, b=b: e.copy(out=AB[:, n, :], in_=ps[b][:]), reads=[psk(b)],
                             writes=[("AB", n)])
                    else:
                        T.op("dve", lambda e, n=n, b=b: e.tensor_copy(out=AB[:, n, :], in_=ps[b][:]), reads=[psk(b)],
                             writes=[("AB", n)])
                nhalf = NCH // HB
                for tg in range(NT):
                    blocks = [(m, hf) for m in range(2) for hf in range(nhalf)]
                    for bi, (m, hf) in enumerate(blocks):
                        src = (cs_in if m == 0 else ssn_in).rearrange("(sc si) t -> si sc t", si=128)[
                            :, hf * HB:(hf + 1) * HB, tg * 512:(tg + 1) * 512]
                        T.dma("sp", blk[:, bi % 4], src, writes=[("blk", bi % 4)])
                    for bi, (m, hf) in enumerate(blocks):
                        for cp in range(2):
                            b = 4 + cp
                            for scl in range(HB):
                                sc = hf * HB + scl
                                first = (bi == 0 and scl == 0)
                                last = (bi == len(blocks) - 1 and scl == HB - 1)
                                T.op("pe", lambda e, m=m, cp=cp, b=b, sc=sc, scl=scl, bi=bi, first=first, last=last:
                                     e.matmul(ps[b][:], lhsT=AB[:, sc, m * 256 + cp * 128:m * 256 + (cp + 1) * 128],
                                              rhs=blk[:, bi % 4, scl, :], start=first, stop=last),
                                     reads=[("AB", sc), ("blk", bi % 4)], writes=[psk(b)])
                    for cp in range(2):
                        b = 4 + cp
                        T.op("act" if cp else "dve",
                             (lambda e, cp=cp, b=b: e.copy(out=yst[:, cp, :], in_=ps[b][:])) if cp else
                             (lambda e, cp=cp, b=b: e.tensor_copy(out=yst[:, cp, :], in_=ps[b][:])),
                             reads=[psk(b)], writes=[("yst", cp)])
                        r0 = 1024 + g * 256 + cp * 128
                        T.dma("sp", oT_s[r0:r0 + 128, tg * 512:(tg + 1) * 512], yst[:, cp, :], reads=[("yst", cp)],
                              writes=[("oTs_f", g, cp, tg)])
    T.barrier()
    if upto == 2:
        return nc

    with ExitStack() as es:
        def sbt(name, shape, dt):
            return es.enter_context(nc.sbuf_tensor("sb_" + name, list(shape), dt))
        xt = sbt("xtc", [128, 4, D], F32)
        hb = sbt("hbc", [128, 4, D], BF16)
        hT = sbt("hTc", [128, 16, 512], BF16)
        ss = sbt("ssc", [128, 4], F32)
        rs = sbt("rsc", [128, 4], F32)
        junk = None
        act = sbt("actc", [128, DFF // 128, 512], BF16)
        wgu = sbt("wguc", [128, 2, 2, 16, 256], BF16)
        wdt = sbt("wdtc", [128, 2, 8, 512], BF16)
        sg = sbt("sgc", [128, 2, 512], F32)
        kTm = sbt("kTm", [128, 16, NMEM], BF16)
        vm = sbt("vm", [128, 2, D], BF16)
        fgain = sbt("fgain", [128, D], F32)
        T.dma("sp", fgain[:], fgain_in, writes=["fgain"])
        bufs = (hb, hT, ss, rs, junk, act, wgu, wdt, sg)
        oTt = sbt("oTt", [128, 16, 512], BF16)
        prob = sbt("prob", [128, 4, NMEM], F32)
        probb = sbt("probb", [128, 4, NMEM], BF16)
        pT_ = hb[:, :, 0:1024].rearrange("p h (a b) -> p h a b", b=512)
        mx = sbt("mx", [128, 4], F32)
        sm = sbt("sm", [128, 4], F32)

        for ms in range(2):
            T.dma("sp", xt[:, ms, :], mem_in[ms * 128:(ms + 1) * 128, :], writes=[("xt", ms)])
        T.op("pool", lambda e: e.memset(xt[:, 2:4, :], 0.0), writes=[("xt", 2), ("xt", 3)])
        norm_transpose(xt, 3, hb, hT, ss, rs, junk)
        kvt = [("k", c0) for c0 in range(0, D, 256)] + [("v", c0) for c0 in range(D, 2 * D, 256)]

        def loadkv(i):
            kind, c0 = kvt[i]
            bi = i % 2
            T.dma("sp", wgu[:, bi, 0, :, :], wtile_ap("wkv", c0, 256), reads=wkeys("wkv"), writes=[("wgu", bi, 0)])

        def compkv(i):
            kind, c0 = kvt[i]
            bi = i % 2
            if kind == "k":
                for c in range(2):
                    fc = c0 // 128 + c
                    b = c
                    for kc in range(16):
                        T.op("pe", lambda e, kc=kc, c=c, b=b: e.matmul(
                            ps[b][:, 0:NMEM], lhsT=wgu[:, bi, 0, kc, c * 128:(c + 1) * 128], rhs=hT[:, kc, 0:NMEM],
                            start=(kc == 0), stop=(kc == 15)),
                            reads=[("wgu", bi, 0), ("hT", kc)], writes=[psk(b)])
                    T.op("act", lambda e, fc=fc, b=b: e.copy(out=kTm[:, fc, :], in_=ps[b][:, 0:NMEM]),
                         reads=[psk(b)], writes=["kTm"])
            else:
                vc = c0 - D
                for ms in range(2):
                    b = 2 + ms
                    for kc in range(16):
                        T.op("pe", lambda e, kc=kc, ms=ms, b=b: e.matmul(
                            ps[b][:, 0:256], lhsT=hT[:, kc, ms * 128:(ms + 1) * 128], rhs=wgu[:, bi, 0, kc, :],
                            start=(kc == 0), stop=(kc == 15)),
                            reads=[("wgu", bi, 0), ("hT", kc)], writes=[psk(b)])
                    T.op("dve", lambda e, ms=ms, b=b: e.tensor_copy(out=vm[:, ms, vc:vc + 256], in_=ps[b][:, 0:256]),
                         reads=[psk(b)], writes=["vm"])

        loadkv(0)
        loadkv(1)
        for i in range(len(kvt)):
            compkv(i)
            if i + 2 < len(kvt):
                loadkv(i + 2)

        for t in range(NT):
            t0 = t * 512
            for ts in range(4):
                T.dma("sp", xt[:, ts, :], x1_s[t0 + ts * 128:t0 + (ts + 1) * 128, :], writes=[("xt", ts)])
            if t == 0:
                T.dma("sp", oTt[:], oT_s.rearrange("(kc ki) s -> ki kc s", ki=128)[:, :, t0:t0 + 512],
                      writes=[("oTt", kc) for kc in range(16)])
            proj_tokmajor(xt, lambda kc, ts: oTt[:, kc, ts * 128:(ts + 1) * 128], lambda kc: ("oTt", kc), "wout", wdt)
            norm_transpose(xt, 2, hb, hT, ss, rs, junk)
            nq = D // 256

            def loadq(i):
                bi = i % 2
                T.dma("sp", wgu[:, bi, 0, :, :], wtile_ap("wq", i * 256, 256), reads=wkeys("wq"),
                      writes=[("wgu", bi, 0)])

            def compq(i):
                bi = i % 2
                for c in range(2):
                    fc = i * 2 + c
                    b = c
                    for kc in range(16):
                        T.op("pe", lambda e, kc=kc, c=c, b=b: e.matmul(
                            ps[b][:], lhsT=wgu[:, bi, 0, kc, c * 128:(c + 1) * 128], rhs=hT[:, kc, :],
                            start=(kc == 0), stop=(kc == 15)),
                            reads=[("wgu", bi, 0), ("hT", kc)], writes=[psk(b)])
                    T.op("act" if c else "dve",
                         (lambda e, fc=fc, b=b: e.copy(out=act[:, fc, :], in_=ps[b][:])) if c else
                         (lambda e, fc=fc, b=b: e.tensor_copy(out=act[:, fc, :], in_=ps[b][:])),
                         reads=[psk(b)], writes=[("act", fc)])

            loadq(0)
            loadq(1)
            for i in range(nq):
                compq(i)
                if i + 2 < nq:
                    loadq(i + 2)
            sc_scale = 512.0 ** -0.5
            units = [(hd, ts) for hd in range(4) for ts in range(4)]
            sbanks = (2, 3, 6, 7)

            def a_scores(u):
                hd, ts = units[u]
                b = sbanks[u % 4]
                for dc in range(4):
                    fc = hd * 4 + dc
                    T.op("pe", lambda e, fc=fc, ts=ts, dc=dc, b=b: e.matmul(
                        ps[b][:, 0:NMEM], lhsT=act[:, fc, ts * 128:(ts + 1) * 128], rhs=kTm[:, fc, :],
                        start=(dc == 0), stop=(dc == 3)),
                        reads=[("act", fc), "kTm"], writes=[psk(b)])

            def a_softmax(u):
                pi = u % 4
                b = sbanks[pi]
                T.op("dve", lambda e: e.tensor_reduce(out=mx[:, pi:pi + 1], in_=ps[b][:, 0:NMEM], axis=AX.X,
                                                      op=ALU.max), reads=[psk(b)], writes=[("mx", pi)])
                T.op("dve", lambda e: e.tensor_scalar(out=mx[:, pi:pi + 1], in0=mx[:, pi:pi + 1],
                                                      scalar1=-sc_scale, scalar2=None, op0=ALU.mult),
                     reads=[("mx", pi)], writes=[("mx", pi)])
                T.op("act", lambda e: e.activation(out=prob[:, pi, :], in_=ps[b][:, 0:NMEM], func=AF.Exp,
                                                   scale=sc_scale, bias=mx[:, pi:pi + 1],
                                                   accum_out=sm[:, pi:pi + 1]),
                     reads=[psk(b), ("mx", pi)], writes=[("prob", pi), ("sm", pi)])
                T.op("dve", lambda e: e.reciprocal(out=sm[:, pi:pi + 1], in_=sm[:, pi:pi + 1]),
                     reads=[("sm", pi)], writes=[("sm", pi)])
                T.op("dve", lambda e: e.tensor_scalar(out=probb[:, pi, :], in0=prob[:, pi, :],
                                                      scalar1=sm[:, pi:pi + 1], scalar2=None, op0=ALU.mult),
                     reads=[("prob", pi), ("sm", pi)], writes=[("probb", pi)])

            def a_transpose(u):
                hd, ts = units[u]
                pi = u % 4
                bt = 4 + u % 2
                pv = ps[bt][:].bitcast(BF16)
                for mc in range(2):
                    T.op("pe", lambda e, mc=mc: e.transpose(
                        out=pv[:, mc * 128:(mc + 1) * 128], in_=probb[:, pi, mc * 128:(mc + 1) * 128],
                        identity=identb[:]), reads=[("probb", pi), "identb"], writes=[psk(bt)])
                T.op("act", lambda e: e.copy(
                    out=pT_[:, hd, :, ts * 128:(ts + 1) * 128],
                    in_=pv[:, 0:256].rearrange("p (a b) -> p a b", b=128)),
                    reads=[psk(bt)], writes=[("hb", hd)])

            DEPTH = 3
            for u in range(min(DEPTH, len(units))):
                a_scores(u)
            for u in range(len(units)):
                a_softmax(u)
                if u >= 1:
                    a_transpose(u - 1)
                if u + DEPTH < len(units):
                    a_scores(u + DEPTH)
            a_transpose(len(units) - 1)
            for fc in range(16):
                hd = fc // 4
                b = fc % 2
                for mc in range(2):
                    T.op("pe", lambda e, fc=fc, hd=hd, mc=mc, b=b: e.matmul(
                        ps[b][:], lhsT=vm[:, mc, fc * 128:(fc + 1) * 128], rhs=pT_[:, hd, mc, :],
                        start=(mc == 0), stop=(mc == 1)),
                        reads=["vm", ("hb", hd)], writes=[psk(b)])
                T.op("act" if fc % 2 else "dve",
                     (lambda e, fc=fc, b=b: e.copy(out=oTt[:, fc, :], in_=ps[b][:])) if fc % 2 else
                     (lambda e, fc=fc, b=b: e.tensor_copy(out=oTt[:, fc, :], in_=ps[b][:])),
                     reads=[psk(b)], writes=[("oTt", fc)])
            proj_tokmajor(xt, lambda kc, ts: oTt[:, kc, ts * 128:(ts + 1) * 128], lambda kc: ("oTt", kc), "wo", wdt)
            if t + 1 < NT:
                T.dma("sp", oTt[:], oT_s.rearrange("(kc ki) s -> ki kc s", ki=128)[:, :, t0 + 512:t0 + 1024],
                      writes=[("oTt", kc) for kc in range(16)])
            ffn(xt, 4, "wg2", "wu2", "wd2", bufs)
            for ts in range(4):
                T.op("act", lambda e, ts=ts: e.activation(out=act[:, 4 * ts:4 * ts + 4, :],
                                                          in_=xt[:, ts, :].rearrange("p (a b) -> p a b", b=512),
                                                          func=AF.Square, accum_out=ss[:, ts:ts + 1]),
                     reads=[("xt", ts)], writes=[("act", 4 * ts + j) for j in range(4)] + [("ss", ts)])
                T.op("act", lambda e, ts=ts: e.activation(out=rs[:, ts:ts + 1], in_=ss[:, ts:ts + 1], func=AF.Ln,
                                                          scale=1.0 / D, bias=eps_t[:, 0:1]),
                     reads=[("ss", ts), "eps_t"], writes=[("rs", ts)])
                T.op("act", lambda e, ts=ts: e.activation(out=rs[:, ts:ts + 1], in_=rs[:, ts:ts + 1], func=AF.Exp,
                                                          scale=-0.5),
                     reads=[("rs", ts)], writes=[("rs", ts)])
                sl2 = ts % 2
                ystg = hb[:, 2 * sl2:2 * sl2 + 2, :].bitcast(F32).rearrange("p a b -> p (a b)")
                hk = [("hb", 2 * sl2), ("hb", 2 * sl2 + 1)]
                T.op("dve", lambda e, ts=ts, ystg=ystg: e.scalar_tensor_tensor(out=ystg, in0=xt[:, ts, :],
                                                                               scalar=rs[:, ts:ts + 1], in1=fgain[:],
                                                                               op0=ALU.mult, op1=ALU.mult),
                     reads=[("xt", ts), ("rs", ts), "fgain"], writes=hk)
                T.dma("sp", y_out[t0 + ts * 128:t0 + (ts + 1) * 128, :], ystg, reads=hk,
                      writes=[("y", t, ts)])
    T.barrier()
    return nc


SEQ_PAD = 4096
_NC_CACHE = {}


def _dft_consts(S, n_real):
    idx = np.arange(n_real, dtype=np.float64)
    ang = 2.0 * np.pi * np.outer(idx, idx) / n_real
    cs = np.zeros((S, S), np.float64)
    sn = np.zeros((S, S), np.float64)
    cs[:n_real, :n_real] = np.cos(ang) / np.sqrt(n_real)
    sn[:n_real, :n_real] = -np.sin(ang) / np.sqrt(n_real)
    return cs.astype(ml_dtypes.bfloat16), sn.astype(ml_dtypes.bfloat16)


def _static_consts():
    c = np.arange(256, dtype=np.float64)
    ang = 2.0 * np.pi * np.outer(c, c) / 256.0
    ccsc = np.concatenate([np.cos(ang), np.sin(ang)], axis=1) / 16.0
    ccsc = ccsc.reshape(2, 128, 512).transpose(1, 0, 2).reshape(128, 1024).astype(ml_dtypes.bfloat16)
    k = np.arange(128)[:, None]
    i = np.arange(128)[None, :]
    m = np.zeros((128, 8, 128), np.float32)
    m[:, 0] = (k <= i)
    m[:, 1] = (k >= i)
    m[:, 2] = 1.0
    m[:, 3] = (k > i)
    m[:, 4] = (k < i)
    m[:, 6] = (i > k)
    m[:, 7] = (i < k)
    identb = np.eye(128, dtype=np.float32).astype(ml_dtypes.bfloat16)
    return ccsc, m.reshape(128, 1024), identb


def make_in_maps(S, seqs, mems, n_reals, W):
    ccsc, masks, identb = _static_consts()
    fm = lambda g: np.ascontiguousarray(g.reshape(16, 128).T)
    gains = np.concatenate([fm(W["ffn1_norm"][0]), fm(W["mix_norm"][0]), fm(W["xattn_norm"][0]),
                            fm(W["mem_norm"][0]), fm(W["ffn2_norm"][0])], axis=1).astype(np.float32)
    fgain = np.ascontiguousarray(np.broadcast_to(W["final_norm"].reshape(1, D), (128, D))).astype(np.float32)
    convw = np.ascontiguousarray(W["conv_w"][0].T.reshape(24, 128, 5).transpose(1, 0, 2).reshape(128, 120))
    alog = np.ascontiguousarray(np.broadcast_to(W["a_log"][0].reshape(1, 16), (128, 16))).astype(np.float32)
    dtb = np.ascontiguousarray(np.broadcast_to(W["dt_bias"][0].reshape(1, 16), (128, 16))).astype(np.float32)
    hng = np.ascontiguousarray(np.broadcast_to(W["dn_head_norm"][0].reshape(1, 128), (128, 128))).astype(np.float32)
    common = {
        "wg1": W["ffn1_w_gate"][0], "wu1": W["ffn1_w_up"][0], "wd1": W["ffn1_w_down"][0],
        "win": W["w_in"][0], "wout": W["w_out"][0], "wq": W["xattn_w_q"][0], "wkv": W["xattn_w_kv"][0],
        "wo": W["xattn_w_o"][0], "wg2": W["ffn2_w_gate"][0], "wu2": W["ffn2_w_up"][0], "wd2": W["ffn2_w_down"][0],
        "gains": gains, "fgain": fgain, "convw": convw, "alog": alog, "dtb": dtb, "hng": hng,
        "ccsc": ccsc, "masks": masks, "identb": identb,
    }
    common = {k: np.ascontiguousarray(v) for k, v in common.items()}
    dft = {}
    maps = []
    for x, m, nr in zip(seqs, mems, n_reals):
        if nr not in dft:
            dft[nr] = _dft_consts(S, nr)
        d = dict(common)
        d["x"] = np.ascontiguousarray(x, dtype=np.float32)
        d["mem"] = np.ascontiguousarray(m, dtype=np.float32)
        d["cs"], d["ssn"] = dft[nr]
        tm = np.zeros((128, S), np.float32)
        tm[:, :nr] = 1.0
        d["tmask"] = tm.astype(ml_dtypes.bfloat16)
        maps.append(d)
    return maps


def kernel(**inputs):
    inp = {k: np.asarray(v) for k, v in inputs.items()}
    xp, xs = inp["x_prompt"], inp["x_sample"]
    mp, msm = inp["mem_prompt"], inp["mem_sample"]
    S = SEQ_PAD
    seqs, mems, nreal = [], [], []
    for b in range(xs.shape[0]):
        seqs.append(xs[b])
        mems.append(msm[b])
        nreal.append(xs.shape[1])
    for b in range(xp.shape[0]):
        pad = np.zeros((S, D), np.float32)
        pad[:xp.shape[1]] = xp[b]
        seqs.append(pad)
        mems.append(mp[b])
        nreal.append(xp.shape[1])
    if S not in _NC_CACHE:
        _NC_CACHE[S] = build(S)
    nc = _NC_CACHE[S]
    in_maps = make_in_maps(S, seqs, mems, nreal, inp)
    res = run_bass_kernel_spmd(nc, in_maps, core_ids=list(range(8)))
    ys = np.stack([res.results[b]["y"] for b in range(4)], axis=0).astype(np.float32)
    yp = np.stack([res.results[4 + b]["y"][:xp.shape[1]] for b in range(4)], axis=0).astype(np.float32)
    return (yp, ys)
```
